# Optimizing a Trainium2 kernel written in Bass

```python
import math
import jax, jax.numpy as jnp
from jax import lax
import numpy as np

D_MODEL = 2048
BATCH = 1
SEQ = 16384
DEPTH = 1
DEC_BATCH = 2
DEC_SEQ = 16384
PAST_LEN = 128

HEAD_DIM = 128
DILATED_GROUPS = ((128, 1), (512, 4), (2048, 16))
HEADS_PER_GROUP_A = 8
N_HEADS_A = HEADS_PER_GROUP_A * len(DILATED_GROUPS)
N_HEADS_B = D_MODEL // HEAD_DIM
N_KV_B = 4
D_FF = ((8 * D_MODEL // 3 + 255) // 256) * 256
GRID_W = 64
ROPE_THETA = 10000.0
Q_BLOCK = 128
EPS = 1e-6
NEG_INF = -1e30

A_W = N_HEADS_A * HEAD_DIM
A_OUT = HEADS_PER_GROUP_A * HEAD_DIM
B_Q = N_HEADS_B * HEAD_DIM
B_KV = N_KV_B * HEAD_DIM
IN_COLS = 3 * A_W + B_Q + 2 * B_KV + 2 * D_MODEL

kernel_name = "hybrid_dilated_gqa_gated_encoder"


def rmsnorm(x, g):
    x32 = x.astype(jnp.float32)
    y = x32 * lax.rsqrt(jnp.mean(x32 * x32, axis=-1, keepdims=True) + EPS)
    return (y * g.astype(jnp.float32)).astype(x.dtype)


def rope(x, pos):
    dim = x.shape[-1]
    half = dim // 2
    inv = ROPE_THETA ** (-(jnp.arange(half, dtype=jnp.float32) * 2.0 / dim))
    ang = pos.astype(jnp.float32)[:, None] * inv[None, :]
    cos = jnp.cos(ang)[:, None, :]
    sin = jnp.sin(ang)[:, None, :]
    x32 = x.astype(jnp.float32)
    x1, x2 = x32[..., :half], x32[..., half:]
    return jnp.concatenate([x1 * cos - x2 * sin, x1 * sin + x2 * cos], axis=-1).astype(x.dtype)


def banded_attention(q, k, v, radius):
    n, L, h, dh = q.shape
    blk = radius
    nb = -(-L // blk)
    lp = nb * blk
    qb = jnp.pad(q, ((0, 0), (0, lp - L), (0, 0), (0, 0))).reshape(n, nb, blk, h, dh)
    kv_pad = ((0, 0), (radius, lp - L + radius), (0, 0), (0, 0))
    kp = jnp.pad(k, kv_pad).reshape(n, nb + 2, blk, h, dh)
    vp = jnp.pad(v, kv_pad).reshape(n, nb + 2, blk, h, dh)

    def windows(a):
        return jnp.concatenate([a[:, :-2], a[:, 1:-1], a[:, 2:]], axis=2)

    kw, vw = windows(kp), windows(vp)
    kpos = jnp.arange((nb + 2) * blk) - radius
    kvalid = ((kpos >= 0) & (kpos < L)).reshape(nb + 2, blk)
    kvalid = jnp.concatenate([kvalid[:-2], kvalid[1:-1], kvalid[2:]], axis=1)
    rel = jnp.arange(3 * blk)[None, :] - blk - jnp.arange(blk)[:, None]
    mask = (jnp.abs(rel) <= radius)[None] & kvalid[:, None, :]
    s = jnp.einsum('nbqhd,nbkhd->nbhqk', qb, kw, preferred_element_type=jnp.float32) * (dh ** -0.5)
    s = jnp.where(mask[None, :, None], s, NEG_INF)
    lse = jax.nn.logsumexp(s, axis=-1)
    p = jnp.exp(s - lse[..., None])
    o = jnp.einsum('nbhqk,nbkhd->nbqhd', p.astype(v.dtype), vw).reshape(n, lp, h, dh)[:, :L]
    lse = lse.transpose(0, 1, 3, 2).reshape(n, lp, h)[:, :L]
    return o, lse


def dilated_group(q, k, v, window, dil):
    b, s, h, dh = q.shape
    L = s // dil

    def to_res(a):
        return a.reshape(b, L, dil, h, dh).transpose(0, 2, 1, 3, 4).reshape(b * dil, L, h, dh)

    o, lse = banded_attention(to_res(q), to_res(k), to_res(v), window // (2 * dil))
    o = o.reshape(b, dil, L, h, dh).transpose(0, 2, 1, 3, 4).reshape(b, s, h, dh)
    lse = lse.reshape(b, dil, L, h).transpose(0, 2, 1, 3).reshape(b, s, h)
    return o, lse


def dense_gqa_blocks(q, k, v):
    b, s, hq, dh = q.shape
    hkv = k.shape[2]
    g = hq // hkv
    nq = s // Q_BLOCK
    qb = q.reshape(b, nq, Q_BLOCK, hkv, g, dh).transpose(1, 0, 2, 3, 4, 5)
    scale = dh ** -0.5

    def one(qblk):
        sc = jnp.einsum('bqhgd,bkhd->bhgqk', qblk, k, preferred_element_type=jnp.float32) * scale
        p = jax.nn.softmax(sc, axis=-1)
        return jnp.einsum('bhgqk,bkhd->bqhgd', p.astype(v.dtype), v)

    o = lax.map(one, qb)
    return o.transpose(1, 0, 2, 3, 4, 5).reshape(b, s, hq * dh)


def encoder_layer(x, g_attn, w_in, q_gain_b, k_gain_b, w_a_br, w_b_br, w_o, g_ffn, w_gate_up, w_down):
    b, s, _ = x.shape
    h = rmsnorm(x, g_attn)
    proj = h @ w_in
    cuts = np.cumsum([A_W, A_W, A_W, B_Q, B_KV, B_KV, D_MODEL]).tolist()
    qa, ka, va, qb, kb, vb, ga, gb = jnp.split(proj, cuts, axis=-1)

    t = jnp.arange(s)
    qa = rope(qa.reshape(b, s, N_HEADS_A, HEAD_DIM), t)
    ka = rope(ka.reshape(b, s, N_HEADS_A, HEAD_DIM), t)
    va = va.reshape(b, s, N_HEADS_A, HEAD_DIM)
    outs, lses = [], []
    for gi, (window, dil) in enumerate(DILATED_GROUPS):
        sl = slice(gi * HEADS_PER_GROUP_A, (gi + 1) * HEADS_PER_GROUP_A)
        o_g, lse_g = dilated_group(qa[:, :, sl], ka[:, :, sl], va[:, :, sl], window, dil)
        outs.append(o_g)
        lses.append(lse_g)
    wts = jax.nn.softmax(jnp.stack(lses, axis=0), axis=0)
    ya = jnp.einsum('gbsh,gbshd->bshd', wts, jnp.stack(outs, axis=0).astype(jnp.float32))
    ya = ya.astype(x.dtype).reshape(b, s, A_OUT)

    n_rows = s // GRID_W
    row = jnp.repeat(jnp.arange(n_rows), GRID_W)
    col = jnp.tile(jnp.arange(GRID_W), n_rows)
    half = HEAD_DIM // 2

    def axial(a):
        return jnp.concatenate([rope(a[..., :half], row), rope(a[..., half:], col)], axis=-1)

    qb = axial(rmsnorm(qb.reshape(b, s, N_HEADS_B, HEAD_DIM), q_gain_b))
    kb = axial(rmsnorm(kb.reshape(b, s, N_KV_B, HEAD_DIM), k_gain_b))
    vb = vb.reshape(b, s, N_KV_B, HEAD_DIM)
    yb = dense_gqa_blocks(qb, kb, vb)

    ya_p = ya @ w_a_br
    yb_p = yb @ w_b_br
    merged = (jax.nn.sigmoid(ga.astype(jnp.float32)) * ya_p.astype(jnp.float32)
              + jax.nn.sigmoid(gb.astype(jnp.float32)) * yb_p.astype(jnp.float32)).astype(x.dtype)
    x = x + merged @ w_o

    h2 = rmsnorm(x, g_ffn)
    gate, up = jnp.split(h2 @ w_gate_up, 2, axis=-1)
    x = x + (jax.nn.silu(gate) * up) @ w_down
    return x


def trunk(x, g_attn, w_in, q_gain_b, k_gain_b, w_a_br, w_b_br, w_o, g_ffn, w_gate_up, w_down, g_final):
    for l in range(DEPTH):
        x = encoder_layer(x, g_attn[l], w_in[l], q_gain_b[l], k_gain_b[l], w_a_br[l], w_b_br[l],
                          w_o[l], g_ffn[l], w_gate_up[l], w_down[l])
    return rmsnorm(x, g_final)


def setup_inputs(seed: int = 0) -> dict:
    key = jax.random.key(seed)
    ks = jax.random.split(key, 14)
    f32 = jnp.float32

    def w(k, shape, fan_in):
        return jax.random.normal(k, shape, f32) * (fan_in ** -0.5)

    def gain(k, shape):
        return 1.0 + 0.02 * jax.random.normal(k, shape, f32)

    return {
        "x_prompt": jax.random.normal(ks[0], (BATCH, SEQ, D_MODEL), f32),
        "x_sample": jax.random.normal(ks[1], (DEC_BATCH, DEC_SEQ, D_MODEL), f32),
        "g_attn": gain(ks[2], (DEPTH, D_MODEL)),
        "w_in": w(ks[3], (DEPTH, D_MODEL, IN_COLS), D_MODEL),
        "q_gain_b": gain(ks[4], (DEPTH, HEAD_DIM)),
        "k_gain_b": gain(ks[5], (DEPTH, HEAD_DIM)),
        "w_a_br": w(ks[6], (DEPTH, A_OUT, D_MODEL), A_OUT),
        "w_b_br": w(ks[7], (DEPTH, B_Q, D_MODEL), B_Q),
        "w_o": w(ks[8], (DEPTH, D_MODEL, D_MODEL), D_MODEL),
        "g_ffn": gain(ks[9], (DEPTH, D_MODEL)),
        "w_gate_up": w(ks[10], (DEPTH, D_MODEL, 2 * D_FF), D_MODEL),
        "w_down": w(ks[11], (DEPTH, D_FF, D_MODEL), D_FF),
        "g_final": gain(ks[12], (D_MODEL,)),
    }


def reference(x_prompt, x_sample, g_attn, w_in, q_gain_b, k_gain_b, w_a_br, w_b_br, w_o, g_ffn,
              w_gate_up, w_down, g_final):
    y_prompt = trunk(x_prompt, g_attn, w_in, q_gain_b, k_gain_b, w_a_br, w_b_br, w_o, g_ffn,
                     w_gate_up, w_down, g_final)
    y_sample = trunk(x_sample, g_attn, w_in, q_gain_b, k_gain_b, w_a_br, w_b_br, w_o, g_ffn,
                     w_gate_up, w_down, g_final)
    return (y_prompt, y_sample)
```

```python
import numpy as np
import ml_dtypes
from contextlib import ExitStack
import concourse.bass as bass
import concourse.mybir as mybir
from concourse.bass_utils import run_bass_kernel_spmd

F32 = mybir.dt.float32
BF16 = mybir.dt.bfloat16
AF = mybir.ActivationFunctionType
ALU = mybir.AluOpType
AX = mybir.AxisListType

D = 2048
KC = 16
HD = 128
DFF = 5632
FC = DFF // 128
EPS = 1e-6
THETA = 10000.0
GRID_W = 64
A_W = 3072
C_QA, C_KA, C_VA, C_QB, C_KB, C_VB, C_GA, C_GB = 0, 3072, 6144, 9216, 11264, 11776, 12288, 14336
DILS = (1, 4, 16)
NEG = -30000.0


class Cfg:
    def __init__(self, S=16384, NSEQ=3, NCORES=8, debug=(), phases="0KPABO"):
        self.S, self.NSEQ, self.NC = S, NSEQ, NCORES
        self.CH = S // NCORES
        assert self.CH == 2048
        self.EXT = self.CH + 2048
        self.debug = tuple(debug)
        self.phases = phases


class Ctr:
    def __init__(self, nc, es, name):
        self.sem = es.enter_context(nc.semaphore(name))
        self.n = 0

    def inc(self, instr, amt=1):
        instr.then_inc(self.sem, amt)
        self.n += amt
        return self.n


class Tok:
    __slots__ = ("ctr", "val", "eng", "fence")

    def __init__(self, ctr, val, eng, fence):
        self.ctr, self.val, self.eng, self.fence = ctr, val, eng, fence


class Buf:
    def __init__(self, S, t, name):
        self.t = t
        self.name = name
        self.ready = {}
        self.readers = {}
        self.S = S
        self._ld = None
        self.pending = None

    def __getitem__(self, k):
        return self.t[k]

    def dctr(self):
        if self._ld is None:
            self._ld = self.S.take_ctr()
        return self._ld


def _merge(d, tok):
    k = id(tok.ctr)
    o = d.get(k)
    if o is None or o.val < tok.val or (o.val == tok.val and tok.fence and not o.fence):
        d[k] = tok


class Sched:
    def __init__(self, nc, es):
        self.nc, self.es = nc, es
        self.PE, self.ACT, self.DVE, self.POOL, self.SP = nc.tensor, nc.scalar, nc.vector, nc.gpsimd, nc.sync
        self.prog = {}
        for nm, e in (("pe", self.PE), ("act", self.ACT), ("dve", self.DVE), ("pool", self.POOL), ("sp", self.SP)):
            self.prog[id(e)] = Ctr(nc, es, "prog_" + nm)
        self.waited = {}
        self.pool = []
        self.live = []
        self.pend = {id(e): ([], []) for e in (self.PE, self.ACT, self.DVE, self.POOL, self.SP)}
        self.nctr = 0

    def new_ctr(self, name):
        self.nctr += 1
        return Ctr(self.nc, self.es, name + f"_{self.nctr}")

    def take_ctr(self):
        c = self.pool.pop() if self.pool else self.new_ctr("dq")
        return c

    def end_phase(self):
        for b in self.live:
            if b._ld is not None:
                self.wait(self.SP, Tok(b._ld, b._ld.n, None, False))
        self.phase_barrier()
        for b in self.live:
            if b._ld is not None:
                self.pool.append(b._ld)
                b._ld = None
        self.live = []

    def buf(self, stack, name, shape, dt, psum=False):
        if psum:
            t = stack.enter_context(self.nc.psum_tensor(name, list(shape), dt))
        else:
            t = stack.enter_context(self.nc.sbuf_tensor(name, list(shape), dt))
        b = Buf(self, t, name)
        self.live.append(b)
        return b

    def wait(self, eng, tok):
        if tok.eng is eng and not tok.fence:
            return
        key = (id(eng), tok.ctr)
        if self.waited.get(key, 0) >= tok.val:
            return
        eng.wait_ge(tok.ctr.sem, tok.val)
        self.waited[key] = tok.val

    def _pre(self, eng, reads, writes, skip_ctr=None):
        for b in reads:
            assert b.pending is None or b.pending is eng, b.name
            for t in b.ready.values():
                self.wait(eng, t)
        for b in writes:
            assert b.pending is None or b.pending is eng, b.name
            for t in b.readers.values():
                self.wait(eng, t)
            for t in b.ready.values():
                if skip_ctr is not None and t.ctr is skip_ctr:
                    continue
                self.wait(eng, t)

    def op(self, eng, instr_fn, reads=(), writes=(), inc=True, fence=False):
        self._pre(eng, reads, writes)
        ins = instr_fn()
        pr, pw = self.pend[id(eng)]
        pr.extend(reads)
        pw.extend(writes)
        if not inc:
            for b in writes:
                b.pending = eng
            return None
        ctr = self.prog[id(eng)]
        tok = Tok(ctr, ctr.inc(ins), eng, fence)
        for b in pr:
            _merge(b.readers, tok)
        for b in pw:
            b.ready = {id(ctr): tok}
            b.readers = {}
            b.pending = None
        pr.clear()
        pw.clear()
        return tok

    def phase_barrier(self):
        engs = (self.PE, self.ACT, self.DVE, self.POOL)
        for f in engs:
            c = self.prog[id(f)]
            if c.n > 0:
                self.wait(self.SP, Tok(c, c.n, f, True))
        csp = self.prog[id(self.SP)]
        self.SP.sem_inc(csp.sem, 1)
        csp.n += 1
        for e in engs:
            for f in engs + (self.SP,):
                if e is f:
                    continue
                c = self.prog[id(f)]
                if c.n > 0:
                    self.wait(e, Tok(c, c.n, f, True))

    def dma(self, q, out, in_, reads=(), writes=(), ctr=None, extra_wait=()):
        if ctr is None:
            ctr = (writes[0] if writes else reads[0]).dctr()
        self._pre(q, reads, writes, skip_ctr=ctr)
        for t in extra_wait:
            self.wait(q, t)
        ins = q.dma_start(out=out, in_=in_)
        tok = Tok(ctr, ctr.inc(ins, 16), None, False)
        for b in reads:
            _merge(b.readers, tok)
        for b in writes:
            b.ready = {id(ctr): tok}
            b.readers = {}
        return tok


def rope_tables_B(S):
    t = np.arange(S)
    row = (t // GRID_W).astype(np.float32)
    col = (t % GRID_W).astype(np.float32)
    inv = (THETA ** (-(np.arange(32, dtype=np.float32) * 2.0 / 64))).astype(np.float32)
    angr = (row[:, None] * inv[None, :]).astype(np.float32)
    angc = (col[:, None] * inv[None, :]).astype(np.float32)
    cos = np.concatenate([np.cos(angr), np.cos(angc)], axis=1)
    sin = np.concatenate([np.sin(angr), np.sin(angc)], axis=1)
    return np.ascontiguousarray(np.concatenate([cos, sin], axis=1).astype(np.float32))


def rope_tables_A(tpos):
    inv = (THETA ** (-(np.arange(64, dtype=np.float32) * 2.0 / 128))).astype(np.float32)
    ang = (tpos.astype(np.float32)[:, None] * inv[None, :]).astype(np.float32)
    return np.ascontiguousarray(np.concatenate([np.cos(ang), np.sin(ang)], axis=1).astype(np.float32))


def build(cfg):
    nc = bass.Bass("TRN2", target_bir_lowering=False)
    S, NSEQ, CH, EXT = cfg.S, cfg.NSEQ, cfg.CH, cfg.EXT
    dbg = cfg.debug

    def din(name, shape, dt=F32):
        return nc.dram_tensor(name, list(shape), dt, kind="ExternalInput").ap()

    def dscr(name, shape, dt=BF16):
        kind = "ExternalOutput" if name in dbg else "Internal"
        return nc.dram_tensor(name, list(shape), dt, kind=kind).ap()

    io = dict(
        xall=din("xall", [NSEQ, S, D]),
        xext=din("xext", [NSEQ, EXT, D]),
        w_in=din("w_in", [D, 16384]),
        w_a_br=din("w_a_br", [1024, D]),
        w_b_br=din("w_b_br", [D, D]),
        w_o=din("w_o", [D, D]),
        w_gate_up=din("w_gate_up", [D, 2 * DFF]),
        w_down=din("w_down", [DFF, D]),
        g_attn=din("g_attn", [1, D]),
        g_ffn=din("g_ffn", [1, D]),
        g_final=din("g_final", [1, D]),
        q_gain=din("q_gain", [1, HD]),
        k_gain=din("k_gain", [1, HD]),
        ident=din("ident", [128, 128], BF16),
        ropeB=din("ropeB", [S, 128]),
        ropeA=din("ropeA", [EXT, 128]),
    )
    y = nc.dram_tensor("y", [NSEQ, CH, D], F32, kind="ExternalOutput").ap()
    io["y"] = y

    scr = dict(
        win=dscr("win_bf", [D, 16384]),
        wabr=dscr("wabr_bf", [1024, D]),
        wbbr=dscr("wbbr_bf", [D, D]),
        wo=dscr("wo_bf", [D, D]),
        wgu=dscr("wgu_bf", [D, 2 * DFF]),
        wdn=dscr("wdn_bf", [DFF, D]),
        KbT=dscr("KbT", [NSEQ, 4, 128, S]),
        Vb=dscr("Vb", [NSEQ, 4, 128, S // 128, 128]),
    )

    NCH = [CH // (128 * d_) + 1 for d_ in DILS]
    scr.update(
        QaT=dscr("QaT", [24, 128, CH]),
        KaT=dscr("KaT", [24, 128, EXT]),
        Va0=dscr("Va0", [DILS[0] * NCH[0], 128, 1024]),
        Va1=dscr("Va1", [DILS[1] * NCH[1], 128, 1024]),
        Va2=dscr("Va2", [DILS[2] * NCH[2], 128, 1024]),
        QbT=dscr("QbT", [16, 128, CH]),
        gT=dscr("gT", [32, 128, CH]),
        yaT=dscr("yaT", [8, 128, CH]),
        ybT=dscr("ybT", [16, 128, CH]),
    )
    io["ropeBo"] = din("ropeBo", [CH, 128])
    io["kbias"] = din("kbias", [128, sum(d_ * n_ for d_, n_ in zip(DILS, NCH))])
    io["masks"] = din("masks", [128, 2, 512], BF16)

    with ExitStack() as es:
        G = dict()
        G["S"] = Sched(nc, es)
        G["p0"] = Ctr(nc, es, "p0")
        if "0" in cfg.phases:
            phase_0(nc, cfg, io, scr, G)
        if "K" in cfg.phases:
            phase_K(nc, cfg, io, scr, G)
        for sq in range(NSEQ):
            if "P" in cfg.phases:
                phase_P(nc, cfg, io, scr, G, sq)
            if "A" in cfg.phases:
                phase_A(nc, cfg, io, scr, G, sq)
            if "B" in cfg.phases:
                phase_B(nc, cfg, io, scr, G, sq)
            if "O" in cfg.phases:
                phase_O(nc, cfg, io, scr, G, sq)
        nc.sync.wait_ge(G["p0"].sem, G["p0"].n)
    return nc


def phase_0(nc, cfg, io, scr, G):
    p0 = G["p0"]
    pairs = [("w_in", "win"), ("w_a_br", "wabr"), ("w_b_br", "wbbr"), ("w_o", "wo"),
             ("w_gate_up", "wgu"), ("w_down", "wdn")]
    for a, b in pairs:
        src, dst = io[a], scr[b]
        R = src.shape[0]
        for r0 in range(0, R, 128):
            p0.inc(nc.gpsimd.dma_start(out=dst[r0:r0 + 128, :], in_=src[r0:r0 + 128, :]), 16)


def phase_K(nc, cfg, io, scr, G):
    S, NSEQ = cfg.S, cfg.NSEQ
    NT = S // 128
    N = NSEQ * NT
    Sc = G["S"]
    PE, ACT, DVE, SP, POOL = Sc.PE, Sc.ACT, Sc.DVE, Sc.SP, Sc.POOL
    with ExitStack() as es:
        def sb(name, shape, dt):
            return Sc.buf(es, "k_" + name, shape, dt)

        def ps(name, shape, dt):
            return Sc.buf(es, "k_" + name, shape, dt, psum=True)

        wkv = sb("wkv", [128, KC, 1024], BF16)
        gcol = sb("gcol", [128, KC], F32)
        kg = sb("kg", [128, 128], F32)
        ident = sb("ident", [128, 128], BF16)
        xt = [sb(f"xt{i}", [128, D], F32) for i in range(3)]
        rb = [sb(f"rb{i}", [128, 128], F32) for i in range(3)]
        junk = sb("junk", [128, D], BF16)
        ssq = [sb(f"ssq{i}", [128, 1], F32) for i in range(3)]
        rstd = [sb(f"rstd{i}", [128, 1], F32) for i in range(3)]
        hb = [sb(f"hb{i}", [128, D], BF16) for i in range(2)]
        hT = [sb(f"hT{i}", [128, KC, 128], BF16) for i in range(2)]
        kf = [sb(f"kf{i}", [128, 512], F32) for i in range(2)]
        sk = [sb(f"sk{i}", [128, 4], F32) for i in range(2)]
        rk = [sb(f"rk{i}", [128, 4], F32) for i in range(2)]
        vst = [sb(f"vst{i}", [128, 4, 4, 128], BF16) for i in range(2)]
        kn = sb("kn", [128, 512], F32)
        t1 = sb("t1", [128, 256], F32)
        t2 = sb("t2", [128, 256], F32)
        kr = [sb(f"kr{i}", [128, 512], BF16) for i in range(2)]
        kst = [sb(f"kst{i}", [128, 4, 512], BF16) for i in range(2)]
        cneg = sb("cneg", [128, 4], F32)
        tp = [ps(f"tp{i}", [128, D], BF16) for i in range(2)]
        psk = ps("psk", [128, 512], F32)
        psv = ps("psv", [128, 512], F32)
        tk = ps("tk", [128, 512], BF16)

        w_in = io["w_in"]
        for kc in range(KC):
            Sc.dma(POOL, wkv[:, kc, :], w_in[kc * 128:(kc + 1) * 128, C_KB:C_KB + 1024], writes=(wkv,))
        with nc.allow_non_contiguous_dma(reason="tiny gain column load"):
            Sc.dma(SP, gcol[:], io["g_attn"].rearrange("o (kc p) -> p (o kc)", p=128), writes=(gcol,))
        Sc.dma(SP, kg[:], io["k_gain"].broadcast_to([128, 128]), writes=(kg,))
        Sc.dma(SP, ident[:], io["ident"], writes=(ident,))
        Sc.op(POOL, lambda: POOL.memset(cneg[:], -0.5), writes=(cneg,))
        for kc in range(KC):
            Sc.op(DVE, lambda: DVE.tensor_scalar(out=wkv[:, kc, :], in0=wkv[:, kc, :], scalar1=gcol[:, kc:kc + 1],
                                                 scalar2=None, op0=ALU.mult), reads=(gcol, wkv), writes=(wkv,),
                  inc=(kc == KC - 1))

        def stage_A(i):
            s, j = divmod(i, NT)
            b = i % 3
            Sc.dma(SP, xt[b][:], io["xall"][s, j * 128:(j + 1) * 128, :], writes=(xt[b],))
            Sc.dma(SP, rb[b][:], io["ropeB"][j * 128:(j + 1) * 128, :], writes=(rb[b],))
            Sc.op(ACT, lambda: ACT.activation(out=junk[:], in_=xt[b][:], func=AF.Square, accum_out=ssq[b][:]),
                  reads=(xt[b],), writes=(junk, ssq[b]), fence=True)
            Sc.op(POOL, lambda: POOL.tensor_scalar(out=rstd[b][:], in0=ssq[b][:], scalar1=float(D * EPS), scalar2=None,
                                                   op0=ALU.add), reads=(ssq[b],), writes=(rstd[b],), fence=True)
            Sc.op(POOL, lambda: POOL.tensor_tensor(out=rstd[b][:], in0=rstd[b][:], in1=cneg[:, 0:1], op=ALU.pow),
                  reads=(rstd[b], cneg), writes=(rstd[b],), fence=True)
            Sc.op(DVE, lambda: DVE.tensor_scalar(out=hb[i % 2][:], in0=xt[b][:], scalar1=rstd[b][:, 0:1],
                                                 scalar2=float(np.sqrt(D)), op0=ALU.mult, op1=ALU.mult),
                  reads=(xt[b], rstd[b]), writes=(hb[i % 2],))

        def stage_B(i):
            b = i % 2
            for kc in range(KC):
                Sc.op(PE, lambda: PE.transpose(out=tp[b][:, kc * 128:(kc + 1) * 128],
                                               in_=hb[b][:, kc * 128:(kc + 1) * 128], identity=ident[:]),
                      reads=(hb[b], ident), writes=(tp[b],), inc=(kc == KC - 1))
            Sc.op(ACT, lambda: ACT.copy(out=hT[b][:].rearrange("p k t -> p (k t)"), in_=tp[b][:]),
                  reads=(tp[b],), writes=(hT[b],))

        def stage_C(i):
            s, j = divmod(i, NT)
            b = i % 2
            for kc in range(KC):
                Sc.op(PE, lambda: PE.matmul(psk[:], lhsT=hT[b][:, kc, :], rhs=wkv[:, kc, 0:512], start=(kc == 0),
                                            stop=(kc == KC - 1)), reads=(hT[b], wkv), writes=(psk,), inc=(kc == KC - 1))
            for kc in range(KC):
                Sc.op(PE, lambda: PE.matmul(psv[:], lhsT=hT[b][:, kc, :], rhs=wkv[:, kc, 512:1024], start=(kc == 0),
                                            stop=(kc == KC - 1)), reads=(hT[b], wkv), writes=(psv,), inc=(kc == KC - 1))
            Sc.op(ACT, lambda: ACT.copy(out=kf[b][:], in_=psk[:]), reads=(psk,), writes=(kf[b],))
            for h in range(4):
                Sc.op(ACT, lambda: ACT.activation(out=junk[:, h * 128:(h + 1) * 128], in_=psk[:, h * 128:(h + 1) * 128],
                                                  func=AF.Square, accum_out=sk[b][:, h:h + 1]),
                      reads=(psk,), writes=(junk, sk[b]), fence=True, inc=(h == 3))
            bt, pos = divmod(i, 4)
            Sc.op(ACT, lambda: ACT.copy(out=vst[bt % 2][:, :, pos, :], in_=psv[:].rearrange("p (kv d) -> p kv d", kv=4)),
                  reads=(psv,), writes=(vst[bt % 2],))
            if pos == 3:
                Sc.dma(SP, scr["Vb"][s, :, :, j - 3:j + 1, :].rearrange("kv k c d -> k kv (c d)"),
                       vst[bt % 2][:].rearrange("k kv c d -> k kv (c d)"), reads=(vst[bt % 2],))

        def stage_D(i):
            b = i % 2
            r = rb[i % 3]
            Sc.op(POOL, lambda: POOL.tensor_scalar(out=rk[b][:], in0=sk[b][:], scalar1=1.0 / HD, scalar2=EPS,
                                                   op0=ALU.mult, op1=ALU.add), reads=(sk[b],), writes=(rk[b],), fence=True)
            Sc.op(POOL, lambda: POOL.tensor_tensor(out=rk[b][:], in0=rk[b][:], in1=cneg[:], op=ALU.pow),
                  reads=(rk[b], cneg), writes=(rk[b],), fence=True)
            k3 = kf[b][:].rearrange("p (h d) -> p h d", h=4)
            kn3 = kn[:].rearrange("p (h d) -> p h d", h=4)
            Sc.op(DVE, lambda: DVE.tensor_tensor(out=kn3, in0=k3, in1=rk[b][:].unsqueeze(2).broadcast_to([128, 4, 128]),
                                                 op=ALU.mult), reads=(kf[b], rk[b]), writes=(kn,))
            Sc.op(DVE, lambda: DVE.tensor_tensor(out=kn3, in0=kn3, in1=kg[:].unsqueeze(1).broadcast_to([128, 4, 128]),
                                                 op=ALU.mult), reads=(kn, kg), writes=(kn,))
            v5 = kn[:].rearrange("p (h a b i) -> p h a b i", h=4, a=2, b=2)
            x1, x2 = v5[:, :, :, 0, :], v5[:, :, :, 1, :]
            o5 = kr[b][:].rearrange("p (h a b i) -> p h a b i", h=4, a=2, b=2)
            o1, o2 = o5[:, :, :, 0, :], o5[:, :, :, 1, :]
            cos = r[:, 0:64].rearrange("p (a i) -> p a i", a=2).unsqueeze(1).broadcast_to([128, 4, 2, 32])
            sin = r[:, 64:128].rearrange("p (a i) -> p a i", a=2).unsqueeze(1).broadcast_to([128, 4, 2, 32])
            t1v = t1[:].rearrange("p (h a i) -> p h a i", h=4, a=2)
            t2v = t2[:].rearrange("p (h a i) -> p h a i", h=4, a=2)
            TT = DVE.tensor_tensor
            Sc.op(DVE, lambda: TT(out=t1v, in0=x1, in1=cos, op=ALU.mult), reads=(kn, r), writes=(t1,), fence=True)
            Sc.op(DVE, lambda: TT(out=t2v, in0=x2, in1=sin, op=ALU.mult), reads=(kn, r), writes=(t2,), fence=True)
            Sc.op(DVE, lambda: TT(out=o1, in0=t1v, in1=t2v, op=ALU.subtract), reads=(t1, t2), writes=(kr[b],))
            Sc.op(DVE, lambda: TT(out=t1v, in0=x1, in1=sin, op=ALU.mult), reads=(kn, r), writes=(t1,), fence=True)
            Sc.op(DVE, lambda: TT(out=t2v, in0=x2, in1=cos, op=ALU.mult), reads=(kn, r), writes=(t2,), fence=True)
            Sc.op(DVE, lambda: TT(out=o2, in0=t1v, in1=t2v, op=ALU.add), reads=(t1, t2), writes=(kr[b],))

        def stage_E(i):
            s, j = divmod(i, NT)
            b = i % 2
            bt, pos = divmod(i, 4)
            for h in range(4):
                Sc.op(PE, lambda: PE.transpose(out=tk[:, h * 128:(h + 1) * 128], in_=kr[b][:, h * 128:(h + 1) * 128],
                                               identity=ident[:]), reads=(kr[b], ident), writes=(tk,), inc=(h == 3))
            Sc.op(ACT, lambda: ACT.copy(out=kst[bt % 2][:, :, pos * 128:(pos + 1) * 128],
                                        in_=tk[:].rearrange("p (h t) -> p h t", h=4)), reads=(tk,), writes=(kst[bt % 2],))
            if pos == 3:
                t0 = (j - 3) * 128
                Sc.dma(SP, scr["KbT"][s, :, :, t0:t0 + 512].rearrange("h d t -> d h t"), kst[bt % 2][:],
                       reads=(kst[bt % 2],))

        for step in range(-2, N + 1):
            if 0 <= step + 2 < N:
                stage_A(step + 2)
            if 0 <= step + 1 < N:
                stage_B(step + 1)
            if 0 <= step < N:
                stage_C(step)
                stage_D(step)
            if 0 <= step - 1 < N:
                stage_E(step - 1)
        for bb in vst + kst:
            for t in bb.readers.values():
                Sc.wait(SP, t)
        if "dbgK" in cfg.debug:
            L = N - 1
            items = [("xt", xt[L % 3]), ("ssq", ssq[L % 3]), ("rstd", rstd[L % 3]), ("hb", hb[L % 2]),
                     ("kf", kf[L % 2]), ("sk", sk[L % 2]), ("rk", rk[L % 2]), ("kn", kn), ("kr", kr[L % 2]),
                     ("gcol", gcol), ("rb", rb[L % 3])]
            dc = Sc.new_ctr("dbgk")
            for nm, bf in items:
                ap = bf[:]
                t = nc.dram_tensor("dbgK_" + nm, list(ap.shape), ap.dtype, kind="ExternalOutput").ap()
                tok = Sc.dma(SP, t, ap, reads=(bf,), ctr=dc)
            Sc.wait(SP, tok)
        Sc.end_phase()


def va_chunks(cfg, g):
    Dl = DILS[g]
    n = cfg.CH // (128 * Dl)
    out = []
    for r in range(Dl):
        for c in range(n + 1):
            out.append((r, c, 1024 - 64 * Dl + 128 * Dl * c + r))
    return out


def ka_tiles(g):
    return {0: list(range(7, 25)), 1: list(range(6, 26)), 2: list(range(0, 32))}[g]


def phase_P(nc, cfg, io, scr, G, s):
    CH, EXT = cfg.CH, cfg.EXT
    Sc = G["S"]
    PE, ACT, DVE, SP, POOL = Sc.PE, Sc.ACT, Sc.DVE, Sc.SP, Sc.POOL
    NXT = EXT // 128
    with ExitStack() as es:
        def sb(name, shape, dt, st=es):
            return Sc.buf(st, f"p{s}_" + name, shape, dt)

        def ps(name, shape, dt, st=es):
            return Sc.buf(st, f"p{s}_" + name, shape, dt, psum=True)

        hT = sb("hT", [128, KC, EXT], BF16)
        ident = sb("ident", [128, 128], BF16)
        cneg = sb("cneg", [128, 4], F32)
        qg = sb("qg", [128, 128], F32)
        Sc.dma(SP, ident[:], io["ident"], writes=(ident,))
        Sc.dma(SP, qg[:], io["q_gain"].broadcast_to([128, 128]), writes=(qg,))
        Sc.op(POOL, lambda: POOL.memset(cneg[:], -0.5), writes=(cneg,))

        with ExitStack() as e1:
            gbc = sb("gbc", [128, D], F32, e1)
            xt = [sb(f"xt{i}", [128, D], F32, e1) for i in range(2)]
            junk = sb("junk", [128, D], BF16, e1)
            ssq = [sb(f"ssq{i}", [128, 1], F32, e1) for i in range(2)]
            rstd = [sb(f"rstd{i}", [128, 1], F32, e1) for i in range(2)]
            hb = [sb(f"hb{i}", [128, D], BF16, e1) for i in range(2)]
            tp = [ps(f"tp{i}", [128, D], BF16, e1) for i in range(2)]
            Sc.dma(SP, gbc[:], io["g_attn"].broadcast_to([128, D]), writes=(gbc,))
            Sc.op(DVE, lambda: DVE.tensor_scalar(out=gbc[:], in0=gbc[:], scalar1=float(np.sqrt(D)), scalar2=None,
                                                 op0=ALU.mult), reads=(gbc,), writes=(gbc,))

            def p1_A(j):
                b = j % 2
                Sc.dma(SP, xt[b][:], io["xext"][s, j * 128:(j + 1) * 128, :], writes=(xt[b],))
                Sc.op(ACT, lambda: ACT.activation(out=junk[:], in_=xt[b][:], func=AF.Square, accum_out=ssq[b][:]),
                      reads=(xt[b],), writes=(junk, ssq[b]), fence=True)
                Sc.op(POOL, lambda: POOL.tensor_scalar(out=rstd[b][:], in0=ssq[b][:], scalar1=float(D * EPS),
                                                       scalar2=None, op0=ALU.add), reads=(ssq[b],), writes=(rstd[b],),
                      fence=True)
                Sc.op(POOL, lambda: POOL.tensor_tensor(out=rstd[b][:], in0=rstd[b][:], in1=cneg[:, 0:1], op=ALU.pow),
                      reads=(rstd[b], cneg), writes=(rstd[b],), fence=True)
                Sc.op(DVE, lambda: DVE.scalar_tensor_tensor(out=hb[b][:], in0=xt[b][:], scalar=rstd[b][:, 0:1], in1=gbc[:],
                                                            op0=ALU.mult, op1=ALU.mult),
                      reads=(xt[b], rstd[b], gbc), writes=(hb[b],))

            def p1_B(j):
                b = j % 2
                for kc in range(KC):
                    Sc.op(PE, lambda: PE.transpose(out=tp[b][:, kc * 128:(kc + 1) * 128],
                                                   in_=hb[b][:, kc * 128:(kc + 1) * 128], identity=ident[:]),
                          reads=(hb[b], ident), writes=(tp[b],), inc=(kc == KC - 1))
                Sc.op(ACT, lambda: ACT.copy(out=hT[:, :, j * 128:(j + 1) * 128],
                                            in_=tp[b][:].rearrange("p (k t) -> p k t", k=KC)),
                      reads=(tp[b],), writes=(hT,))

            p1_A(0)
            for j in range(NXT):
                if j + 1 < NXT:
                    p1_A(j + 1)
                p1_B(j)
            Sc.phase_barrier()

        wblk = [sb(f"w{i}", [128, KC, 512], BF16) for i in range(2)]
        ra = [sb(f"ra{i}", [128, 128], F32) for i in range(3)]
        pf = sb("pf", [128, 512], F32)
        kn = sb("kn", [128, 512], F32)
        sq4 = [sb(f"sq4{i}", [128, 4], F32) for i in range(2)]
        rq4 = [sb(f"rq4{i}", [128, 4], F32) for i in range(2)]
        junk2 = sb("junk2", [128, 512], BF16)
        t1 = sb("t1", [128, 256], F32)
        t2 = sb("t2", [128, 256], F32)
        pr = [sb(f"pr{i}", [128, 512], BF16) for i in range(2)]
        stg = [sb(f"stg{i}", [128, 4, 512], BF16) for i in range(2)]
        vstg = [sb(f"vstg{i}", [128, 512], BF16) for i in range(3)]
        pm = [ps(f"pm{i}", [128, 512], F32) for i in range(3)]
        tk = [ps(f"tk{i}", [128, 512], BF16) for i in range(2)]
        w_in = io["w_in"]
        cnt = dict(pm=0, ra=0, pr=0, stg=0, vs=0, tk=0, q4=0)

        blocks = []
        for q in range(6):
            blocks.append((C_QA + 512 * q, "qa", q))
        for q in range(6):
            blocks.append((C_KA + 512 * q, "ka", q))
        for q in range(6):
            blocks.append((C_VA + 512 * q, "va", q))
        for q in range(4):
            blocks.append((C_QB + 512 * q, "qb", q))
        for q in range(8):
            blocks.append((C_GA + 512 * q, "g", q))

        def load_w(n):
            c0 = blocks[n][0]
            wb = wblk[n % 2]
            for k4 in range(4):
                Sc.dma(POOL, wb[:, 4 * k4:4 * k4 + 4, :],
                       w_in[512 * k4:512 * (k4 + 1), c0:c0 + 512].rearrange("(kc p) c -> p kc c", p=128), writes=(wb,))

        def mm_tile(wb, tok_ap_fn):
            p = pm[cnt["pm"] % 3]
            cnt["pm"] += 1
            for kc in range(KC):
                Sc.op(PE, lambda: PE.matmul(p[:], lhsT=tok_ap_fn(kc), rhs=wb[:, kc, :], start=(kc == 0),
                                            stop=(kc == KC - 1)), reads=(hT, wb), writes=(p,), inc=(kc == KC - 1))
            return p

        def rope_to(src, src_buf, tab, nsub, out_buf):
            i_ = 64 // nsub
            v5 = src.rearrange("p (h a b i) -> p h a b i", h=4, a=nsub, b=2)
            x1, x2 = v5[:, :, :, 0, :], v5[:, :, :, 1, :]
            o5 = out_buf[:].rearrange("p (h a b i) -> p h a b i", h=4, a=nsub, b=2)
            o1, o2 = o5[:, :, :, 0, :], o5[:, :, :, 1, :]
            cos = tab[:, 0:64].rearrange("p (a i) -> p a i", a=nsub).unsqueeze(1).broadcast_to([128, 4, nsub, i_])
            sin = tab[:, 64:128].rearrange("p (a i) -> p a i", a=nsub).unsqueeze(1).broadcast_to([128, 4, nsub, i_])
            t1v = t1[:].rearrange("p (h a i) -> p h a i", h=4, a=nsub)
            t2v = t2[:].rearrange("p (h a i) -> p h a i", h=4, a=nsub)
            TT = DVE.tensor_tensor
            Sc.op(DVE, lambda: TT(out=t1v, in0=x1, in1=cos, op=ALU.mult), reads=(src_buf, tab), writes=(t1,), fence=True)
            Sc.op(DVE, lambda: TT(out=t2v, in0=x2, in1=sin, op=ALU.mult), reads=(src_buf, tab), writes=(t2,), fence=True)
            Sc.op(DVE, lambda: TT(out=o1, in0=t1v, in1=t2v, op=ALU.subtract), reads=(t1, t2), writes=(out_buf,))
            Sc.op(DVE, lambda: TT(out=t1v, in0=x1, in1=sin, op=ALU.mult), reads=(src_buf, tab), writes=(t1,), fence=True)
            Sc.op(DVE, lambda: TT(out=t2v, in0=x2, in1=cos, op=ALU.mult), reads=(src_buf, tab), writes=(t2,), fence=True)
            Sc.op(DVE, lambda: TT(out=o2, in0=t1v, in1=t2v, op=ALU.add), reads=(t1, t2), writes=(out_buf,))

        def transpose_stage(prb, st, pos):
            t = tk[cnt["tk"] % 2]
            cnt["tk"] += 1
            for h in range(4):
                Sc.op(PE, lambda: PE.transpose(out=t[:, h * 128:(h + 1) * 128], in_=prb[:, h * 128:(h + 1) * 128],
                                               identity=ident[:]), reads=(prb, ident), writes=(t,), inc=(h == 3))
            Sc.op(ACT, lambda: ACT.copy(out=st[:, :, pos * 128:(pos + 1) * 128],
                                        in_=t[:].rearrange("p (h t) -> p h t", h=4)), reads=(t,), writes=(st,))

        def do_rope_block(n, kind, q):
            wb = wblk[n % 2]
            if kind == "qa":
                tiles = list(range(8, 24))
                dst, toff = scr["QaT"], -1024
            else:
                tiles = ka_tiles(q // 2)
                dst, toff = scr["KaT"], 0
            nt = len(tiles)
            p_next = mm_tile(wb, lambda kc: hT[:, kc, tiles[0] * 128:(tiles[0] + 1) * 128])
            for ti, j in enumerate(tiles):
                tab = ra[cnt["ra"] % 3]
                cnt["ra"] += 1
                Sc.dma(SP, tab[:], io["ropeA"][j * 128:(j + 1) * 128, :], writes=(tab,))
                p = p_next
                if ti + 1 < nt:
                    jn = tiles[ti + 1]
                    p_next = mm_tile(wb, lambda kc: hT[:, kc, jn * 128:(jn + 1) * 128])
                prb = pr[cnt["pr"] % 2]
                cnt["pr"] += 1
                rope_to(p[:], p, tab, 1, prb)
                pos = ti % 4
                st = stg[cnt["stg"] % 2]
                transpose_stage(prb, st, pos)
                if pos == 3 or ti == nt - 1:
                    j0 = tiles[ti - pos]
                    u0 = j0 * 128 + toff
                    w_ = (pos + 1) * 128
                    Sc.dma(SP, dst[4 * q:4 * q + 4, :, u0:u0 + w_].rearrange("h d t -> d h t"), st[:, :, 0:w_], reads=(st,))
                    cnt["stg"] += 1

        def do_va_block(n, q):
            wb = wblk[n % 2]
            g = q // 2
            Dl = DILS[g]
            dst = scr[f"Va{g}"]
            for ci, (r, c, u) in enumerate(va_chunks(cfg, g)):
                p = mm_tile(wb, lambda kc: hT[:, kc, u:u + 127 * Dl + 1:Dl])
                v = vstg[cnt["vs"] % 3]
                cnt["vs"] += 1
                Sc.op(ACT, lambda: ACT.copy(out=v[:], in_=p[:]), reads=(p,), writes=(v,))
                Sc.dma(SP, dst[ci, :, (q % 2) * 512:(q % 2) * 512 + 512], v[:], reads=(v,))

        def do_qb_block(n, q):
            wb = wblk[n % 2]
            p_next = mm_tile(wb, lambda kc: hT[:, kc, 8 * 128:9 * 128])
            for ti in range(16):
                j = 8 + ti
                tab = ra[cnt["ra"] % 3]
                cnt["ra"] += 1
                Sc.dma(SP, tab[:], io["ropeBo"][ti * 128:(ti + 1) * 128, :], writes=(tab,))
                p = p_next
                if ti + 1 < 16:
                    p_next = mm_tile(wb, lambda kc: hT[:, kc, (j + 1) * 128:(j + 2) * 128])
                sq = sq4[cnt["q4"] % 2]
                rq = rq4[cnt["q4"] % 2]
                cnt["q4"] += 1
                Sc.op(ACT, lambda: ACT.copy(out=pf[:], in_=p[:]), reads=(p,), writes=(pf,))
                for h in range(4):
                    Sc.op(ACT, lambda: ACT.activation(out=junk2[:, h * 128:(h + 1) * 128], in_=p[:, h * 128:(h + 1) * 128],
                                                      func=AF.Square, accum_out=sq[:, h:h + 1]),
                          reads=(p,), writes=(junk2, sq), fence=True, inc=(h == 3))
                Sc.op(POOL, lambda: POOL.tensor_scalar(out=rq[:], in0=sq[:], scalar1=1.0 / HD, scalar2=EPS, op0=ALU.mult,
                                                       op1=ALU.add), reads=(sq,), writes=(rq,), fence=True)
                Sc.op(POOL, lambda: POOL.tensor_tensor(out=rq[:], in0=rq[:], in1=cneg[:], op=ALU.pow),
                      reads=(rq, cneg), writes=(rq,), fence=True)
                p3 = pf[:].rearrange("p (h d) -> p h d", h=4)
                kn3 = kn[:].rearrange("p (h d) -> p h d", h=4)
                Sc.op(DVE, lambda: DVE.tensor_tensor(out=kn3, in0=p3, in1=rq[:].unsqueeze(2).broadcast_to([128, 4, 128]),
                                                     op=ALU.mult), reads=(pf, rq), writes=(kn,))
                Sc.op(DVE, lambda: DVE.tensor_tensor(out=kn3, in0=kn3, in1=qg[:].unsqueeze(1).broadcast_to([128, 4, 128]),
                                                     op=ALU.mult), reads=(kn, qg), writes=(kn,))
                prb = pr[cnt["pr"] % 2]
                cnt["pr"] += 1
                rope_to(kn[:], kn, tab, 2, prb)
                pos = ti % 4
                st = stg[cnt["stg"] % 2]
                transpose_stage(prb, st, pos)
                if pos == 3:
                    w0 = (ti - 3) * 128
                    Sc.dma(SP, scr["QbT"][4 * q:4 * q + 4, :, w0:w0 + 512].rearrange("h d t -> d h t"), st[:],
                           reads=(st,))
                    cnt["stg"] += 1

        def do_gate_block(n, q):
            wb = wblk[n % 2]
            for cc in range(4):
                for tb in range(CH // 512):
                    p = pm[cnt["pm"] % 3]
                    cnt["pm"] += 1
                    for kc in range(KC):
                        Sc.op(PE, lambda: PE.matmul(p[:], lhsT=wb[:, kc, cc * 128:(cc + 1) * 128],
                                                    rhs=hT[:, kc, 1024 + tb * 512:1024 + (tb + 1) * 512], start=(kc == 0),
                                                    stop=(kc == KC - 1)), reads=(hT, wb), writes=(p,), inc=(kc == KC - 1))
                    v = vstg[cnt["vs"] % 3]
                    cnt["vs"] += 1
                    Sc.op(ACT, lambda: ACT.activation(out=v[:], in_=p[:], func=AF.Sigmoid), reads=(p,), writes=(v,))
                    Sc.dma(SP, scr["gT"][4 * q + cc, :, tb * 512:(tb + 1) * 512], v[:], reads=(v,))

        sel = cfg.pblocks if getattr(cfg, "pblocks", None) is not None else range(len(blocks))
        sel = list(sel)
        blocks = [blocks[i] for i in sel]
        load_w(0)
        for n, (c0, kind, q) in enumerate(blocks):
            if n + 1 < len(blocks):
                load_w(n + 1)
            if kind in ("qa", "ka"):
                do_rope_block(n, kind, q)
            elif kind == "va":
                do_va_block(n, q)
            elif kind == "qb":
                do_qb_block(n, q)
            else:
                do_gate_block(n, q)
        Sc.end_phase()


def phase_A(nc, cfg, io, scr, G, s):
    CH, EXT = cfg.CH, cfg.EXT
    Sc = G["S"]
    PE, ACT, DVE, SP, POOL = Sc.PE, Sc.ACT, Sc.DVE, Sc.SP, Sc.POOL
    scale = float(HD ** -0.5)
    NCH = [CH // (128 * d_) + 1 for d_ in DILS]
    goff = [0, DILS[0] * NCH[0], DILS[0] * NCH[0] + DILS[1] * NCH[1]]
    nkb = goff[2] + DILS[2] * NCH[2]
    with ExitStack() as es:
        def sb(name, shape, dt):
            return Sc.buf(es, f"a{s}_" + name, shape, dt)

        def ps(name, shape, dt):
            return Sc.buf(es, f"a{s}_" + name, shape, dt, psum=True)

        qa = sb("qa", [128, 4, CH], BF16)
        ka = sb("ka", [128, 4, EXT], BF16)
        va = sb("va", [128, 32, 512], BF16)
        kbias = sb("kbias", [128, nkb], F32)
        masks = sb("masks", [128, 2, 512], BF16)
        ones = sb("ones", [128, 128], BF16)
        accn = sb("accn", [128, 4, CH], F32)
        accd = sb("accd", [128, 4, CH], F32)
        pt = [sb(f"pt{i}", [128, 512], BF16) for i in range(3)]
        yst = sb("yst", [128, 4, CH], BF16)
        st = [ps(f"st{i}", [128, 512], F32) for i in range(2)]
        num = [ps(f"num{i}", [128, 512], F32) for i in range(2)]
        den = [ps(f"den{i}", [128, 512], F32) for i in range(2)]
        Sc.dma(SP, kbias[:], io["kbias"], writes=(kbias,))
        Sc.dma(SP, masks[:], io["masks"], writes=(masks,))
        Sc.op(DVE, lambda: DVE.memset(ones[:], 1.0), writes=(ones,))
        agroups = getattr(cfg, "agroups", (0, 1, 2))
        for hb in range(2):
            for g in agroups:
                q = 2 * g + hb
                Dl = DILS[g]
                n = CH // (128 * Dl)
                nchunks = Dl * (n + 1)
                Sc.dma(SP, qa[:], scr["QaT"][4 * q:4 * q + 4].rearrange("h d t -> d h t"), writes=(qa,))
                Sc.dma(SP, ka[:], scr["KaT"][4 * q:4 * q + 4].rearrange("h d t -> d h t"), writes=(ka,))
                Sc.dma(SP, va[:, 0:nchunks, :], scr[f"Va{g}"][:, :, hb * 512:(hb + 1) * 512].rearrange("c k n -> k c n"),
                       writes=(va,))
                units = [(r, i, ci) for r in range(Dl) for i in range(n) for ci in range(2)]

                def emit_st(u):
                    r, i, ci = units[u]
                    c = i + ci
                    w0 = 128 * Dl * i + r
                    qsl = slice(w0, w0 + 127 * Dl + 1, Dl)
                    uu = 1024 - 64 * Dl + 128 * Dl * c + r
                    ksl = slice(uu, uu + 127 * Dl + 1, Dl)
                    stb = st[u % 2]
                    for h in range(4):
                        Sc.op(PE, lambda: PE.matmul(stb[:, h * 128:(h + 1) * 128], lhsT=ka[:, h, ksl], rhs=qa[:, h, qsl],
                                                    start=True, stop=True, skip_group_check=True),
                              reads=(ka, qa), writes=(stb,), inc=(h == 3))

                emit_st(0)
                for u, (r, i, ci) in enumerate(units):
                    c = i + ci
                    chunk = r * (n + 1) + c
                    w0 = 128 * Dl * i + r
                    qsl = slice(w0, w0 + 127 * Dl + 1, Dl)
                    tq = (r * n + i)
                    nm, dn = num[tq % 2], den[tq % 2]
                    stb = st[u % 2]
                    p = pt[u % 3]
                    Sc.op(ACT, lambda: ACT.activation(out=p[:], in_=stb[:], func=AF.Exp, scale=scale,
                                                      bias=kbias[:, goff[g] + chunk:goff[g] + chunk + 1]),
                          reads=(stb, kbias), writes=(p,))
                    Sc.op(DVE, lambda: DVE.tensor_tensor(out=p[:], in0=p[:], in1=masks[:, ci, :], op=ALU.mult),
                          reads=(p, masks), writes=(p,))
                    if u + 1 < len(units):
                        emit_st(u + 1)
                    for h in range(4):
                        Sc.op(PE, lambda: PE.matmul(nm[:, h * 128:(h + 1) * 128], lhsT=va[:, chunk, h * 128:(h + 1) * 128],
                                                    rhs=p[:, h * 128:(h + 1) * 128], start=(ci == 0 and h == 0),
                                                    stop=(ci == 1 and h == 3), skip_group_check=True),
                              reads=(va, p), writes=(nm,), inc=False)
                    Sc.op(PE, lambda: PE.matmul(dn[:], lhsT=ones[:], rhs=p[:], start=(ci == 0), stop=(ci == 1)),
                          reads=(ones, p), writes=(dn,), inc=True)
                    if ci == 1:
                        an = accn[:, :, qsl]
                        ad = accd[:, :, qsl]
                        nm3 = nm[:].rearrange("p (h t) -> p h t", h=4)
                        dn3 = dn[:].rearrange("p (h t) -> p h t", h=4)
                        if g == agroups[0]:
                            Sc.op(ACT, lambda: ACT.copy(out=an, in_=nm3), reads=(nm,), writes=(accn,))
                            Sc.op(ACT, lambda: ACT.copy(out=ad, in_=dn3), reads=(dn,), writes=(accd,))
                        else:
                            Sc.op(DVE, lambda: DVE.tensor_tensor(out=an, in0=nm3, in1=an, op=ALU.add), reads=(nm, accn),
                                  writes=(accn,))
                            Sc.op(DVE, lambda: DVE.tensor_tensor(out=ad, in0=dn3, in1=ad, op=ALU.add), reads=(dn, accd),
                                  writes=(accd,))
            a2 = accn[:].rearrange("p h t -> p (h t)")
            d2 = accd[:].rearrange("p h t -> p (h t)")
            Sc.op(DVE, lambda: DVE.reciprocal(out=d2, in_=d2), reads=(accd,), writes=(accd,))
            Sc.op(DVE, lambda: DVE.tensor_tensor(out=yst[:].rearrange("p h t -> p (h t)"), in0=a2, in1=d2, op=ALU.mult),
                  reads=(accn, accd), writes=(yst,))
            Sc.dma(SP, scr["yaT"][4 * hb:4 * hb + 4].rearrange("h d t -> d h t"), yst[:], reads=(yst,))
        Sc.end_phase()


def phase_B(nc, cfg, io, scr, G, s):
    CH, S = cfg.CH, cfg.S
    NT = S // 128
    QT = 256
    NQ = CH // QT
    Sc = G["S"]
    PE, ACT, DVE, SP, POOL = Sc.PE, Sc.ACT, Sc.DVE, Sc.SP, Sc.POOL
    scale = float(HD ** -0.5)
    with ExitStack() as es:
        def sb(name, shape, dt):
            return Sc.buf(es, f"b{s}_" + name, shape, dt)

        def ps(name, shape, dt):
            return Sc.buf(es, f"b{s}_" + name, shape, dt, psum=True)

        kbt = [sb(f"kbt{i}", [128, S], BF16) for i in range(2)]
        vb = [sb(f"vb{i}", [128, NT, 128], BF16) for i in range(2)]
        qb = [sb(f"qb{i}", [128, 4, CH], BF16) for i in range(2)]
        ones = sb("ones", [128, 128], BF16)
        NPT = 4
        pt = [sb(f"pt{i}", [128, 4 * QT], BF16) for i in range(NPT)]
        rden = sb("rden", [128, 4 * QT], F32)
        dacc = sb("dacc", [128, 4 * QT], F32)
        ones32 = sb("ones32", [128, 128], F32)
        DEN_PE_EVERY = getattr(cfg, "den_pe_every", 4)
        yq = [sb(f"yq{i}", [128, 4, QT], BF16) for i in range(2)]
        st = [ps(f"st{i}", [128, 4 * QT], F32) for i in range(2)]
        num = ps("num", [128, 4 * QT], F32)
        den = ps("den", [128, 4 * QT], F32)
        Sc.op(DVE, lambda: DVE.memset(ones[:], 1.0), writes=(ones,))
        Sc.op(DVE, lambda: DVE.memset(ones32[:], 1.0), writes=(ones32,))

        def load_kv(kv):
            i = kv % 2
            Sc.dma(SP, kbt[i][:], scr["KbT"][s, kv], writes=(kbt[i],))
            Sc.dma(SP, vb[i][:], scr["Vb"][s, kv], writes=(vb[i],))
            Sc.dma(SP, qb[i][:], scr["QbT"][4 * kv:4 * kv + 4].rearrange("h d t -> d h t"), writes=(qb[i],))

        units = [(kv, qt, c) for kv in range(4) for qt in range(NQ) for c in range(NT)]

        def emit_st(u):
            kv, qt, c = units[u]
            stb = st[u % 2]
            k_, q_ = kbt[kv % 2], qb[kv % 2]
            for h in range(4):
                Sc.op(PE, lambda: PE.matmul(stb[:, h * QT:(h + 1) * QT], lhsT=k_[:, c * 128:(c + 1) * 128],
                                            rhs=q_[:, h, qt * QT:(qt + 1) * QT], start=True, stop=True,
                                            skip_group_check=True),
                      reads=(k_, q_), writes=(stb,), inc=(h == 3))

        load_kv(0)
        emit_st(0)
        emit_st(1)
        for u, (kv, qt, c) in enumerate(units):
            if qt == 0 and c == 0 and kv + 1 < 4:
                load_kv(kv + 1)
            stb = st[u % 2]
            p = pt[u % NPT]
            Sc.op(ACT, lambda: ACT.activation(out=p[:], in_=stb[:], func=AF.Exp, scale=scale), reads=(stb,), writes=(p,))
            v_ = vb[kv % 2]
            for h in range(4):
                Sc.op(PE, lambda: PE.matmul(num[:, h * QT:(h + 1) * QT], lhsT=v_[:, c, :], rhs=p[:, h * QT:(h + 1) * QT],
                                            start=(c == 0 and h % 2 == 0), stop=(c == NT - 1 and h % 2 == 1),
                                            skip_group_check=True),
                      reads=(v_, p), writes=(num,), inc=(h == 3 and c % DEN_PE_EVERY != 0))
            if c % DEN_PE_EVERY == 0:
                for hf in range(2):
                    Sc.op(PE, lambda: PE.matmul(den[:, hf * 512:(hf + 1) * 512], lhsT=ones[:], rhs=p[:, hf * 512:(hf + 1) * 512],
                                                start=(c == 0), stop=False, skip_group_check=True),
                          reads=(ones, p), writes=(den,), inc=(hf == 1))
            else:
                if c == 1:
                    Sc.op(DVE, lambda: DVE.tensor_copy(out=dacc[:], in_=p[:]), reads=(p,), writes=(dacc,))
                else:
                    Sc.op(DVE, lambda: DVE.tensor_tensor(out=dacc[:], in0=dacc[:], in1=p[:], op=ALU.add), reads=(dacc, p),
                          writes=(dacc,))
            if u + 2 < len(units):
                emit_st(u + 2)
            if c == NT - 1:
                for hf in range(2):
                    Sc.op(PE, lambda: PE.matmul(den[:, hf * 512:(hf + 1) * 512], lhsT=ones32[:],
                                                rhs=dacc[:, hf * 512:(hf + 1) * 512], start=False, stop=True,
                                                skip_group_check=True),
                          reads=(ones32, dacc), writes=(den,), inc=(hf == 1))
            if c == NT - 1:
                y = yq[(kv * NQ + qt) % 2]
                Sc.op(DVE, lambda: DVE.reciprocal(out=rden[:], in_=den[:]), reads=(den,), writes=(rden,))
                Sc.op(DVE, lambda: DVE.tensor_tensor(out=y[:].rearrange("p h t -> p (h t)"), in0=num[:], in1=rden[:],
                                                     op=ALU.mult), reads=(num, rden), writes=(y,))
                Sc.dma(SP, scr["ybT"][4 * kv:4 * kv + 4, :, qt * QT:(qt + 1) * QT].rearrange("h d t -> d h t"), y[:],
                       reads=(y,))
        Sc.end_phase()


def phase_O(nc, cfg, io, scr, G, s):
    CH = cfg.CH
    TB = 512
    NSB = CH // TB
    Sc = G["S"]
    PE, ACT, DVE, SP, POOL = Sc.PE, Sc.ACT, Sc.DVE, Sc.SP, Sc.POOL
    with ExitStack() as es:
        def sb(name, shape, dt):
            return Sc.buf(es, f"o{s}_" + name, shape, dt)

        def ps(name, shape, dt):
            return Sc.buf(es, f"o{s}_" + name, shape, dt, psum=True)

        U = sb("U", [128, 28672], BF16)
        ya = U[:, 0:4096].rearrange("p (j t) -> p j t", j=8)
        yb = U[:, 4096:12288].rearrange("p (j t) -> p j t", j=16)
        gt = U[:, 12288:28672].rearrange("p (j t) -> p j t", j=32)
        aT = U[:, 0:FC * TB].rearrange("p (f t) -> p f t", f=FC)
        mT = sb("mT", [128, KC, TB], BF16)
        x1 = [sb(f"x1{i}", [128, D], F32) for i in range(4)]
        h2T = sb("h2T", [128, KC, TB], BF16)
        hb2 = sb("hb2", [128, D], BF16)
        junk = sb("junk", [128, D], BF16)
        gffn = sb("gffn", [128, D], F32)
        gfin = sb("gfin", [128, D], F32)
        wr = [sb(f"wr{i}", [128, 12288], BF16) for i in range(2)]
        tmp1 = sb("tmp1", [128, TB], F32)
        tmp2 = sb("tmp2", [128, TB], F32)
        sg = [sb(f"sg{i}", [128, TB], F32) for i in range(2)]
        ssq = [sb(f"ssq{i}", [128, 1], F32) for i in range(2)]
        rstd = [sb(f"rstd{i}", [128, 1], F32) for i in range(2)]
        cneg = sb("cneg", [128, 4], F32)
        ident = sb("ident", [128, 128], BF16)
        pq = [ps(f"pq{i}", [128, 512], F32) for i in range(4)]
        po = [ps(f"po{i}", [128, 512], F32) for i in range(2)]
        tp = ps("tp", [128, D], BF16)

        Sc.dma(SP, ident[:], io["ident"], writes=(ident,))
        Sc.dma(SP, gffn[:], io["g_ffn"].broadcast_to([128, D]), writes=(gffn,))
        Sc.dma(SP, gfin[:], io["g_final"].broadcast_to([128, D]), writes=(gfin,))
        Sc.op(POOL, lambda: POOL.memset(cneg[:], -0.5), writes=(cneg,))
        for gb_ in (gffn, gfin):
            Sc.op(DVE, lambda: DVE.tensor_scalar(out=gb_[:], in0=gb_[:], scalar1=float(np.sqrt(D)), scalar2=None,
                                                 op0=ALU.mult), reads=(gb_,), writes=(gb_,))
        if s == 0:
            p0 = G["p0"]
            Sc.wait(SP, Tok(p0, p0.n, None, False))
        cnt = dict(n=0)

        def norm_rstd(xb):
            k = cnt["n"] % 2
            cnt["n"] += 1
            Sc.op(ACT, lambda: ACT.activation(out=junk[:], in_=xb[:], func=AF.Square, accum_out=ssq[k][:]),
                  reads=(xb,), writes=(junk, ssq[k]), fence=True)
            Sc.op(POOL, lambda: POOL.tensor_scalar(out=rstd[k][:], in0=ssq[k][:], scalar1=float(D * EPS), scalar2=None,
                                                   op0=ALU.add), reads=(ssq[k],), writes=(rstd[k],), fence=True)
            Sc.op(POOL, lambda: POOL.tensor_tensor(out=rstd[k][:], in0=rstd[k][:], in1=cneg[:, 0:1], op=ALU.pow),
                  reads=(rstd[k], cneg), writes=(rstd[k],), fence=True)
            return rstd[k]

        sched = []
        for sbk in range(NSB):
            for gi in range(4):
                sched.append((sbk, "o1", gi))
            for cb in range(4):
                sched.append((sbk, "o2", cb))
            for fg in range(FC // 2):
                sched.append((sbk, "o4", fg))
            for cb in range(4):
                for fgi in range(2):
                    sched.append((sbk, "o5", (cb, fgi)))

        def wload(k):
            sbk, kind, a = sched[k]
            w = wr[k % 2]
            if kind == "o1":
                c0 = a * 512
                Sc.dma(SP, w[:, 0:4096].rearrange("p (j c) -> p j c", j=8),
                       scr["wabr"][:, c0:c0 + 512].rearrange("(j p) c -> p j c", p=128), writes=(w,))
                Sc.dma(SP, w[:, 4096:12288].rearrange("p (j c) -> p j c", j=16),
                       scr["wbbr"][:, c0:c0 + 512].rearrange("(j p) c -> p j c", p=128), writes=(w,))
            elif kind == "o2":
                c0 = a * 512
                Sc.dma(SP, w[:, 0:8192].rearrange("p (j c) -> p j c", j=16),
                       scr["wo"][:, c0:c0 + 512].rearrange("(j p) c -> p j c", p=128), writes=(w,))
            elif kind == "o4":
                f0 = a * 2
                wv = w[:, 0:8192].rearrange("p (j c) -> p j c", j=16)
                Sc.dma(SP, wv[:, :, 0:256], scr["wgu"][:, f0 * 128:f0 * 128 + 256].rearrange("(j p) c -> p j c", p=128),
                       writes=(w,))
                Sc.dma(SP, wv[:, :, 256:512],
                       scr["wgu"][:, DFF + f0 * 128:DFF + f0 * 128 + 256].rearrange("(j p) c -> p j c", p=128), writes=(w,))
            else:
                cb, fgi = a
                Sc.dma(SP, w[:, 0:22 * 512].rearrange("p (f c) -> p f c", f=22),
                       scr["wdn"][fgi * 22 * 128:(fgi + 1) * 22 * 128, cb * 512:(cb + 1) * 512].rearrange(
                           "(f p) c -> p f c", p=128), writes=(w,))

        def compute(k):
            sbk, kind, a = sched[k]
            w = wr[k % 2]
            t0 = sbk * TB
            if kind == "o1":
                if a == 0:
                    Sc.dma(SP, ya, scr["yaT"][:, :, t0:t0 + TB].rearrange("j d t -> d j t"), writes=(U,))
                    Sc.dma(SP, yb, scr["ybT"][:, :, t0:t0 + TB].rearrange("j d t -> d j t"), writes=(U,))
                    Sc.dma(SP, gt, scr["gT"][:, :, t0:t0 + TB].rearrange("j d t -> d j t"), writes=(U,))
                    for t in range(4):
                        r0 = 1024 + t0 + t * 128
                        Sc.dma(SP, x1[t][:], io["xext"][s, r0:r0 + 128, :], writes=(x1[t],))
                wa = w[:, 0:4096].rearrange("p (j c) -> p j c", j=8)
                wb = w[:, 4096:12288].rearrange("p (j c) -> p j c", j=16)
                for ccl in range(4):
                    cc = 4 * a + ccl
                    pa, pb = pq[2 * (ccl % 2)], pq[2 * (ccl % 2) + 1]
                    for j in range(8):
                        Sc.op(PE, lambda: PE.matmul(pa[:], lhsT=wa[:, j, ccl * 128:(ccl + 1) * 128], rhs=ya[:, j, :],
                                                    start=(j == 0), stop=(j == 7)), reads=(w, U), writes=(pa,), inc=(j == 7))
                    for j in range(16):
                        Sc.op(PE, lambda: PE.matmul(pb[:], lhsT=wb[:, j, ccl * 128:(ccl + 1) * 128], rhs=yb[:, j, :],
                                                    start=(j == 0), stop=(j == 15)), reads=(w, U), writes=(pb,), inc=(j == 15))
                    Sc.op(DVE, lambda: DVE.tensor_tensor(out=tmp1[:], in0=pa[:], in1=gt[:, cc, :], op=ALU.mult),
                          reads=(pa, U), writes=(tmp1,))
                    Sc.op(DVE, lambda: DVE.tensor_tensor(out=tmp2[:], in0=pb[:], in1=gt[:, 16 + cc, :], op=ALU.mult),
                          reads=(pb, U), writes=(tmp2,))
                    Sc.op(DVE, lambda: DVE.tensor_tensor(out=mT[:, cc, :], in0=tmp1[:], in1=tmp2[:], op=ALU.add),
                          reads=(tmp1, tmp2), writes=(mT,))
            elif kind == "o2":
                cb = a
                wo = w[:, 0:8192].rearrange("p (j c) -> p j c", j=16)
                for t in range(4):
                    p = po[t % 2]
                    for j in range(KC):
                        Sc.op(PE, lambda: PE.matmul(p[:], lhsT=mT[:, j, t * 128:(t + 1) * 128], rhs=wo[:, j, :],
                                                    start=(j == 0), stop=(j == KC - 1)), reads=(mT, w), writes=(p,),
                              inc=(j == KC - 1))
                    xs_ = x1[t][:, cb * 512:(cb + 1) * 512]
                    Sc.op(DVE, lambda: DVE.tensor_tensor(out=xs_, in0=p[:], in1=xs_, op=ALU.add), reads=(p, x1[t]),
                          writes=(x1[t],))
                if cb == 3:
                    for t in range(4):
                        r = norm_rstd(x1[t])
                        Sc.op(DVE, lambda: DVE.scalar_tensor_tensor(out=hb2[:], in0=x1[t][:], scalar=r[:, 0:1], in1=gffn[:],
                                                                    op0=ALU.mult, op1=ALU.mult),
                              reads=(x1[t], r, gffn), writes=(hb2,))
                        for kc in range(KC):
                            Sc.op(PE, lambda: PE.transpose(out=tp[:, kc * 128:(kc + 1) * 128],
                                                           in_=hb2[:, kc * 128:(kc + 1) * 128], identity=ident[:]),
                                  reads=(hb2, ident), writes=(tp,), inc=(kc == KC - 1))
                        Sc.op(ACT, lambda: ACT.copy(out=h2T[:, :, t * 128:(t + 1) * 128],
                                                    in_=tp[:].rearrange("p (k t) -> p k t", k=KC)), reads=(tp,), writes=(h2T,))
            elif kind == "o4":
                wv = w[:, 0:8192].rearrange("p (j c) -> p j c", j=16)
                for fl in range(2):
                    f = 2 * a + fl
                    pg, pu = pq[2 * fl], pq[2 * fl + 1]
                    for kc in range(KC):
                        Sc.op(PE, lambda: PE.matmul(pg[:], lhsT=wv[:, kc, fl * 128:(fl + 1) * 128], rhs=h2T[:, kc, :],
                                                    start=(kc == 0), stop=(kc == KC - 1)), reads=(w, h2T), writes=(pg,),
                              inc=(kc == KC - 1))
                    for kc in range(KC):
                        Sc.op(PE, lambda: PE.matmul(pu[:], lhsT=wv[:, kc, 256 + fl * 128:256 + (fl + 1) * 128], rhs=h2T[:, kc, :],
                                                    start=(kc == 0), stop=(kc == KC - 1)), reads=(w, h2T), writes=(pu,),
                              inc=(kc == KC - 1))
                    sgb = sg[f % 2]
                    Sc.op(ACT, lambda: ACT.activation(out=sgb[:], in_=pg[:], func=AF.Silu), reads=(pg,), writes=(sgb,))
                    Sc.op(DVE, lambda: DVE.tensor_tensor(out=aT[:, f, :], in0=pu[:], in1=sgb[:], op=ALU.mult),
                          reads=(pu, sgb), writes=(U,))
            else:
                cb, fgi = a
                wd = w[:, 0:22 * 512].rearrange("p (f c) -> p f c", f=22)
                for fl in range(22):
                    f = fgi * 22 + fl
                    for t in range(4):
                        Sc.op(PE, lambda: PE.matmul(pq[t][:], lhsT=aT[:, f, t * 128:(t + 1) * 128], rhs=wd[:, fl, :],
                                                    start=(f == 0), stop=(f == FC - 1)), reads=(U, w), writes=(pq[t],),
                              inc=(fl == 21 and t == 3))
                if fgi == 1:
                    for t in range(4):
                        xs_ = x1[t][:, cb * 512:(cb + 1) * 512]
                        Sc.op(DVE, lambda: DVE.tensor_tensor(out=xs_, in0=pq[t][:], in1=xs_, op=ALU.add),
                              reads=(pq[t], x1[t]), writes=(x1[t],))
                    if cb == 3:
                        for t in range(4):
                            r = norm_rstd(x1[t])
                            Sc.op(DVE, lambda: DVE.scalar_tensor_tensor(out=x1[t][:], in0=x1[t][:], scalar=r[:, 0:1],
                                                                        in1=gfin[:], op0=ALU.mult, op1=ALU.mult),
                                  reads=(x1[t], r, gfin), writes=(x1[t],))
                            r0 = t0 + t * 128
                            Sc.dma(SP, io["y"][s, r0:r0 + 128, :], x1[t][:], reads=(x1[t],))

        wload(0)
        for k in range(len(sched)):
            if k + 1 < len(sched):
                wload(k + 1)
            compute(k)
        Sc.end_phase()


def make_in_maps(cfg, x_seqs, weights):
    S, NSEQ, CH, EXT = cfg.S, cfg.NSEQ, cfg.CH, cfg.EXT
    ident = np.eye(128, dtype=np.float32).astype(ml_dtypes.bfloat16)
    ropeB = rope_tables_B(S)
    common = dict(
        xall=np.ascontiguousarray(x_seqs),
        w_in=np.ascontiguousarray(weights["w_in"].reshape(D, 16384)),
        w_a_br=np.ascontiguousarray(weights["w_a_br"].reshape(1024, D)),
        w_b_br=np.ascontiguousarray(weights["w_b_br"].reshape(D, D)),
        w_o=np.ascontiguousarray(weights["w_o"].reshape(D, D)),
        w_gate_up=np.ascontiguousarray(weights["w_gate_up"].reshape(D, 2 * DFF)),
        w_down=np.ascontiguousarray(weights["w_down"].reshape(DFF, D)),
        g_attn=np.ascontiguousarray(weights["g_attn"].reshape(1, D)),
        g_ffn=np.ascontiguousarray(weights["g_ffn"].reshape(1, D)),
        g_final=np.ascontiguousarray(weights["g_final"].reshape(1, D)),
        q_gain=np.ascontiguousarray(weights["q_gain_b"].reshape(1, HD)),
        k_gain=np.ascontiguousarray(weights["k_gain_b"].reshape(1, HD)),
        ident=ident,
        ropeB=ropeB,
    )
    maps = []
    for c in range(cfg.NC):
        te = c * CH - 1024
        xext = np.zeros((NSEQ, EXT, D), np.float32)
        lo, hi = max(te, 0), min(te + EXT, S)
        xext[:, lo - te:hi - te, :] = x_seqs[:, lo:hi, :]
        m = dict(common)
        m["xext"] = xext
        m["ropeA"] = rope_tables_A(np.arange(te, te + EXT))
        m["ropeBo"] = np.ascontiguousarray(ropeB[c * CH:(c + 1) * CH])
        kb = []
        for g in range(3):
            for (r, cc, u) in va_chunks(cfg, g):
                tpos = te + u + DILS[g] * np.arange(128)
                kb.append(np.where((tpos >= 0) & (tpos < S), 0.0, NEG).astype(np.float32))
        m["kbias"] = np.ascontiguousarray(np.stack(kb, axis=1))
        a_ = np.arange(128)
        mA = (a_[:, None] >= a_[None, :]).astype(np.float32)
        mB = (a_[:, None] <= a_[None, :]).astype(np.float32)
        m["masks"] = np.ascontiguousarray(
            np.stack([np.tile(mA, (1, 4)), np.tile(mB, (1, 4))], axis=1).astype(ml_dtypes.bfloat16))
        maps.append(m)
    return maps


_NC_CACHE = {}


def kernel(x_prompt, x_sample, g_attn, w_in, q_gain_b, k_gain_b, w_a_br, w_b_br, w_o, g_ffn,
           w_gate_up, w_down, g_final):
    cfg = Cfg()
    x_seqs = np.concatenate([np.asarray(x_prompt, np.float32), np.asarray(x_sample, np.float32)], axis=0)
    weights = dict(g_attn=np.asarray(g_attn), w_in=np.asarray(w_in), q_gain_b=np.asarray(q_gain_b),
                   k_gain_b=np.asarray(k_gain_b), w_a_br=np.asarray(w_a_br), w_b_br=np.asarray(w_b_br),
                   w_o=np.asarray(w_o), g_ffn=np.asarray(g_ffn), w_gate_up=np.asarray(w_gate_up),
                   w_down=np.asarray(w_down), g_final=np.asarray(g_final))
    if "nc" not in _NC_CACHE:
        _NC_CACHE["nc"] = build(cfg)
    nc = _NC_CACHE["nc"]
    maps = make_in_maps(cfg, x_seqs, weights)
    res = run_bass_kernel_spmd(nc, maps, core_ids=list(range(cfg.NC)))
    yfull = np.concatenate([r["y"] for r in res.results], axis=1)
    return (np.ascontiguousarray(yfull[0:1]), np.ascontiguousarray(yfull[1:3]))
```
